# Optimizing a Trainium2 kernel written in Bass

```python
import math
import jax, jax.numpy as jnp
from jax import lax
import numpy as np

D_MODEL = 1024
BATCH = 2
SEQ = 8192
DEPTH = 1
DEC_BATCH = 32
DEC_SEQ = 4
PAST_LEN = 16384
PAGE_SIZE = 128

D_MIX = D_MODEL
N_ATTN_HEADS = 8
HEAD_DIM = 64
ATTN_WIDTH = N_ATTN_HEADS * HEAD_DIM
CONV_CH = D_MIX - ATTN_WIDTH
CONV_W = 3
D_FF = 2816
D_PROJ = 3 * ATTN_WIDTH + N_ATTN_HEADS + 3 * CONV_CH
Q_BLOCK = 128
EPS = 1e-6
FORGET_BIAS_LO = 2.0
FORGET_BIAS_HI = 10.0

kernel_name = 'hymba_fox_shortconv_macaron_step'


def rms_norm(x, g):
    xf = x.astype(jnp.float32)
    y = xf * lax.rsqrt(jnp.mean(xf * xf, axis=-1, keepdims=True) + EPS)
    return (y * g.astype(jnp.float32)).astype(x.dtype)


def swiglu(x, w1, w3, w2):
    return (jax.nn.silu(x @ w1) * (x @ w3)) @ w2


def combined_projection(h, w_in, b_f):
    p = h @ w_in
    A, H, C = ATTN_WIDTH, N_ATTN_HEADS, CONV_CH
    q, k, v, fl, bg, cg, u = jnp.split(
        p, [A, 2 * A, 3 * A, 3 * A + H, 3 * A + H + C, 3 * A + H + 2 * C], axis=-1)
    Bsz, L = h.shape[0], h.shape[1]
    q = q.reshape(Bsz, L, H, HEAD_DIM)
    k = k.reshape(Bsz, L, H, HEAD_DIM)
    v = v.reshape(Bsz, L, H, HEAD_DIM)
    logf = jax.nn.log_sigmoid((fl + b_f).astype(jnp.float32))
    uc = cg * u
    return q, k, v, logf, bg, uc


def fox_prompt(q, k, v, logf):
    Bsz, L, H, Dh = q.shape
    n_blk = L // Q_BLOCK
    scale = 1.0 / math.sqrt(Dh)
    c = lax.cumsum(logf, axis=1)
    c_keys = jnp.transpose(c, (0, 2, 1))[:, :, None, :]
    kpos = jnp.arange(L)
    qb = q.reshape(Bsz, n_blk, Q_BLOCK, H, Dh).transpose(1, 0, 2, 3, 4)
    cb = c.reshape(Bsz, n_blk, Q_BLOCK, H).transpose(1, 0, 2, 3)

    def one_block(args):
        qi, ci, blk = args
        s = jnp.einsum('bqhd,bkhd->bhqk', qi, k).astype(jnp.float32) * scale
        s = s + jnp.transpose(ci, (0, 2, 1))[..., None] - c_keys
        qpos = blk * Q_BLOCK + jnp.arange(Q_BLOCK)
        mask = kpos[None, :] <= qpos[:, None]
        s = jnp.where(mask[None, None], s, -jnp.inf)
        pr = jax.nn.softmax(s, axis=-1).astype(v.dtype)
        return jnp.einsum('bhqk,bkhd->bqhd', pr, v)

    o = lax.map(one_block, (qb, cb, jnp.arange(n_blk)))
    return o.transpose(1, 0, 2, 3, 4).reshape(Bsz, L, H * Dh)


def fox_sample(q, k, v, logf, k_past, v_past, logf_past):
    Bsz, T, H, Dh = q.shape
    P = k_past.shape[1]
    scale = 1.0 / math.sqrt(Dh)
    lp = logf_past.astype(jnp.float32)
    suffix = lax.cumsum(lp, axis=1, reverse=True) - lp
    c_new = lax.cumsum(logf, axis=1)
    c_q = jnp.transpose(c_new, (0, 2, 1))[..., None]
    s_past = jnp.einsum('bqhd,bkhd->bhqk', q, k_past).astype(jnp.float32) * scale
    s_past = s_past + c_q + jnp.transpose(suffix, (0, 2, 1))[:, :, None, :]
    s_new = jnp.einsum('bqhd,bkhd->bhqk', q, k).astype(jnp.float32) * scale
    s_new = s_new + c_q - jnp.transpose(c_new, (0, 2, 1))[:, :, None, :]
    tpos = jnp.arange(T)
    s_new = jnp.where((tpos[None, :] <= tpos[:, None])[None, None], s_new, -jnp.inf)
    pr = jax.nn.softmax(jnp.concatenate([s_past, s_new], axis=-1), axis=-1).astype(v.dtype)
    o = (jnp.einsum('bhqk,bkhd->bqhd', pr[..., :P], v_past)
         + jnp.einsum('bhqk,bkhd->bqhd', pr[..., P:], v))
    return o.reshape(Bsz, T, H * Dh)


def short_conv(u, buf, conv_w):
    L = u.shape[1]
    ext = jnp.concatenate([buf, u], axis=1)
    y = conv_w[0] * ext[:, 0:L]
    for i in range(1, CONV_W):
        y = y + conv_w[i] * ext[:, i:i + L]
    return y, ext[:, L:]


def merge_groups(o_attn, bg, y_conv, g_ao, g_co, w_out):
    conv_out = bg * y_conv
    return jnp.concatenate([rms_norm(o_attn, g_ao), rms_norm(conv_out, g_co)], axis=-1) @ w_out


def setup_inputs(seed: int = 0) -> dict:
    key = jax.random.key(seed)
    ks = jax.random.split(key, 24)
    n_pages = PAST_LEN // PAGE_SIZE
    n_used = DEC_BATCH * n_pages
    n_pool = n_used + n_used // 4
    f32 = jnp.float32

    def nrm(k, shape, scale):
        return jax.random.normal(k, shape, f32) * scale

    def gain(k, n):
        return 1.0 + 0.02 * jax.random.normal(k, (DEPTH, n), f32)

    head_bias = jnp.linspace(FORGET_BIAS_LO, FORGET_BIAS_HI, N_ATTN_HEADS, dtype=f32)
    page_table = jax.random.permutation(ks[0], n_pool)[:n_used].reshape(DEC_BATCH, n_pages).astype(jnp.int32)
    return {
        'x_prompt': nrm(ks[1], (BATCH, SEQ, D_MODEL), 1.0),
        'x_sample': nrm(ks[2], (DEC_BATCH, DEC_SEQ, D_MODEL), 1.0),
        'cache_k': nrm(ks[3], (DEPTH, n_pool, PAGE_SIZE, N_ATTN_HEADS, HEAD_DIM), 1.0),
        'cache_v': nrm(ks[4], (DEPTH, n_pool, PAGE_SIZE, N_ATTN_HEADS, HEAD_DIM), 1.0),
        'cache_logf': jax.nn.log_sigmoid(head_bias + jax.random.normal(ks[5], (DEPTH, n_pool, PAGE_SIZE, N_ATTN_HEADS), f32)),
        'state_conv': nrm(ks[6], (DEPTH, DEC_BATCH, CONV_W - 1, CONV_CH), 1.0),
        'page_table': page_table,
        'g_ffn1': gain(ks[7], D_MODEL),
        'w1_ffn1': nrm(ks[8], (DEPTH, D_MODEL, D_FF), D_MODEL ** -0.5),
        'w3_ffn1': nrm(ks[9], (DEPTH, D_MODEL, D_FF), D_MODEL ** -0.5),
        'w2_ffn1': nrm(ks[10], (DEPTH, D_FF, D_MODEL), D_FF ** -0.5),
        'g_mix': gain(ks[11], D_MODEL),
        'w_in': nrm(ks[12], (DEPTH, D_MODEL, D_PROJ), D_MODEL ** -0.5),
        'b_f': head_bias + 0.3 * jax.random.normal(ks[13], (DEPTH, N_ATTN_HEADS), f32),
        'conv_w': nrm(ks[14], (DEPTH, CONV_W, CONV_CH), CONV_W ** -0.5),
        'g_attn_out': gain(ks[15], ATTN_WIDTH),
        'g_conv_out': gain(ks[16], CONV_CH),
        'w_out': nrm(ks[17], (DEPTH, D_MIX, D_MODEL), D_MIX ** -0.5),
        'g_ffn2': gain(ks[18], D_MODEL),
        'w1_ffn2': nrm(ks[19], (DEPTH, D_MODEL, D_FF), D_MODEL ** -0.5),
        'w3_ffn2': nrm(ks[20], (DEPTH, D_MODEL, D_FF), D_MODEL ** -0.5),
        'w2_ffn2': nrm(ks[21], (DEPTH, D_FF, D_MODEL), D_FF ** -0.5),
        'g_final': 1.0 + 0.02 * jax.random.normal(ks[22], (D_MODEL,), f32),
    }


def reference(x_prompt, x_sample, cache_k, cache_v, cache_logf, state_conv, page_table,
              g_ffn1, w1_ffn1, w3_ffn1, w2_ffn1, g_mix, w_in, b_f, conv_w,
              g_attn_out, g_conv_out, w_out, g_ffn2, w1_ffn2, w3_ffn2, w2_ffn2, g_final):
    xp, xs = x_prompt, x_sample
    n_past = page_table.shape[1] * PAGE_SIZE
    kp_l, vp_l, fp_l, cp_l, ks_l, vs_l, fs_l, cs_l = [], [], [], [], [], [], [], []
    for l in range(DEPTH):
        xp = xp + 0.5 * swiglu(rms_norm(xp, g_ffn1[l]), w1_ffn1[l], w3_ffn1[l], w2_ffn1[l])
        xs = xs + 0.5 * swiglu(rms_norm(xs, g_ffn1[l]), w1_ffn1[l], w3_ffn1[l], w2_ffn1[l])

        q, k, v, logf, bg, uc = combined_projection(rms_norm(xp, g_mix[l]), w_in[l], b_f[l])
        o_attn = fox_prompt(q, k, v, logf)
        y_conv, buf_p = short_conv(uc, jnp.zeros((uc.shape[0], CONV_W - 1, CONV_CH), uc.dtype), conv_w[l])
        xp = xp + merge_groups(o_attn, bg, y_conv, g_attn_out[l], g_conv_out[l], w_out[l])
        kp_l.append(k); vp_l.append(v); fp_l.append(logf); cp_l.append(buf_p)

        q, k, v, logf, bg, uc = combined_projection(rms_norm(xs, g_mix[l]), w_in[l], b_f[l])
        k_past = cache_k[l][page_table].reshape(DEC_BATCH, n_past, N_ATTN_HEADS, HEAD_DIM)
        v_past = cache_v[l][page_table].reshape(DEC_BATCH, n_past, N_ATTN_HEADS, HEAD_DIM)
        f_past = cache_logf[l][page_table].reshape(DEC_BATCH, n_past, N_ATTN_HEADS)
        o_attn = fox_sample(q, k, v, logf, k_past, v_past, f_past)
        y_conv, buf_s = short_conv(uc, state_conv[l].astype(uc.dtype), conv_w[l])
        xs = xs + merge_groups(o_attn, bg, y_conv, g_attn_out[l], g_conv_out[l], w_out[l])
        ks_l.append(k); vs_l.append(v); fs_l.append(logf); cs_l.append(buf_s)

        xp = xp + 0.5 * swiglu(rms_norm(xp, g_ffn2[l]), w1_ffn2[l], w3_ffn2[l], w2_ffn2[l])
        xs = xs + 0.5 * swiglu(rms_norm(xs, g_ffn2[l]), w1_ffn2[l], w3_ffn2[l], w2_ffn2[l])

    y_prompt = rms_norm(xp, g_final)
    y_sample = rms_norm(xs, g_final)
    return (y_prompt, y_sample,
            jnp.stack(kp_l), jnp.stack(vp_l), jnp.stack(fp_l), jnp.stack(cp_l),
            jnp.stack(ks_l), jnp.stack(vs_l), jnp.stack(fs_l), jnp.stack(cs_l))
```

```python
import contextlib
import numpy as np
import concourse.bass as bass
import concourse.mybir as mybir
from concourse.bass_utils import run_bass_kernel_spmd

F32 = mybir.dt.float32
BF16 = mybir.dt.bfloat16
AF = mybir.ActivationFunctionType
ALU = mybir.AluOpType

D = 1024
KC = 8
DFF = 2816
FC = 22
NPT = 2048
EPS = 1e-6
NCORES = 8


class Buf:
    def __init__(self, name):
        self.name = name
        self.w = None
        self.r = {}
        self.dsem = None
        self.dcnt = 0
        self.psum = False
        self.wx = {}


class Prog:
    ENGS = ("sync", "scalar", "vector", "gpsimd", "tensor")

    def __init__(self, nc, stack):
        self.nc = nc
        self.stack = stack
        self.q = {e: [] for e in self.ENGS}
        self.sems = {}
        self.cnt = {e: 0 for e in self.ENGS}
        self.seen = {e: {} for e in self.ENGS}
        for e in self.ENGS:
            self.sems[("eng", e)] = stack.enter_context(nc.semaphore("s_" + e))
        self.nsem = 0
        self.dbufs = []

    def barrier_all(self):
        toks = [(("eng", e), self.cnt[e]) for e in self.ENGS if self.cnt[e] > 0]
        toks += [(b.dsem, b.dcnt) for b in self.dbufs]
        for e in self.ENGS:
            for t in toks:
                self._wait(e, t)

    def new_sem(self, name):
        self.nsem += 1
        key = ("d", name, self.nsem)
        self.sems[key] = self.stack.enter_context(self.nc.semaphore("d%d" % self.nsem))
        return key

    def _wait(self, eng, tok):
        if tok is None:
            return
        key, val = tok
        if eng == "tensor" and key == ("eng", "tensor"):
            return
        if self.seen[eng].get(key, 0) >= val:
            return
        self.seen[eng][key] = val
        sem = self.sems[key]
        self.q[eng].append(lambda e, sem=sem, val=val: e.wait_ge(sem, val))

    def _deps(self, eng, reads, writes):
        for b in reads:
            self._wait(eng, b.w)
            for k, v in b.wx.items():
                self._wait(eng, (k, v))
            if b.psum:
                for k, v in b.r.items():
                    self._wait(eng, (k, v))
        for b in writes:
            self._wait(eng, b.w)
            for k, v in b.wx.items():
                self._wait(eng, (k, v))
            for k, v in b.r.items():
                self._wait(eng, (k, v))

    def _mark(self, tok, reads, writes):
        for b in reads:
            k, v = tok
            if b.r.get(k, 0) < v:
                b.r[k] = v
        for b in writes:
            b.w = tok
            b.r = {}
            b.wx = {}

    def op(self, eng, fn, reads=(), writes=()):
        self._deps(eng, reads, writes)
        self.cnt[eng] += 1
        key = ("eng", eng)
        sem = self.sems[key]
        self.q[eng].append(lambda e, fn=fn, sem=sem: fn(e).then_inc(sem, 1))
        tok = (key, self.cnt[eng])
        self.seen[eng][key] = max(self.seen[eng].get(key, 0), 0)
        self._mark(tok, reads, writes)
        return tok

    def dma(self, eng, fn, dst, reads=(), extra_writes=(), waw=True, src=None):
        if src is not None:
            self._deps(eng, reads, [])
            self._wait(eng, dst.w)
            for k, v in dst.r.items():
                self._wait(eng, (k, v))
            if src.dsem is None:
                src.dsem = self.new_sem(src.name)
                self.dbufs.append(src)
            src.dcnt += 16
            sem = self.sems[src.dsem]
            self.q[eng].append(lambda e, fn=fn, sem=sem: fn(e).then_inc(sem, 16))
            k, v = (src.dsem, src.dcnt)
            for b in reads:
                if b.r.get(k, 0) < v:
                    b.r[k] = v
            if dst.wx.get(k, 0) < v:
                dst.wx[k] = v
            return (k, v)
        if waw:
            self._deps(eng, reads, [dst] + list(extra_writes))
        else:
            self._deps(eng, reads, list(extra_writes))
            for k, v in dst.r.items():
                self._wait(eng, (k, v))
        if dst.dsem is None:
            dst.dsem = self.new_sem(dst.name)
            self.dbufs.append(dst)
        dst.dcnt += 16
        sem = self.sems[dst.dsem]
        self.q[eng].append(lambda e, fn=fn, sem=sem: fn(e).then_inc(sem, 16))
        tok = (dst.dsem, dst.dcnt)
        self._mark(tok, reads, [dst] + list(extra_writes))
        return tok

    def cc(self, fn, dst, reads=()):
        eng = "gpsimd"
        self._deps(eng, reads, [dst])
        if dst.dsem is None:
            dst.dsem = self.new_sem(dst.name)
            self.dbufs.append(dst)
        dst.dcnt += 1
        sem = self.sems[dst.dsem]
        self.q[eng].append(lambda e, fn=fn, sem=sem: fn(e).then_inc(sem))
        tok = (dst.dsem, dst.dcnt)
        self._mark(tok, reads, [dst])
        return tok

    def barrier(self, bufs):
        for b in bufs:
            for e in self.ENGS:
                self._wait(e, b.w)
                for k, v in b.wx.items():
                    self._wait(e, (k, v))
                for k, v in b.r.items():
                    self._wait(e, (k, v))

    def const_group(self, items):
        key = self.new_sem("cg")
        sem = self.sems[key]
        n = 0
        for (eng, fn, b) in items:
            n += 16
            self.q[eng].append(lambda e, fn=fn, sem=sem: fn(e).then_inc(sem, 16))
        for (eng, fn, b) in items:
            b.w = (key, n)
            b.r = {}

    def finish(self, bufs):
        for b in bufs:
            self._wait("sync", b.w)
            for k, v in b.wx.items():
                self._wait("sync", (k, v))

    def build(self):
        nc = self.nc
        with nc.Block() as block:
            @block.sync
            def _(e):
                for f in self.q["sync"]:
                    f(e)

            @block.scalar
            def _(e):
                for f in self.q["scalar"]:
                    f(e)

            @block.vector
            def _(e):
                for f in self.q["vector"]:
                    f(e)

            @block.gpsimd
            def _(e):
                for f in self.q["gpsimd"]:
                    f(e)

            @block.tensor
            def _(e):
                for f in self.q["tensor"]:
                    f(e)


class Ring:
    def __init__(self, slots):
        self.slots = slots
        self.i = 0

    def next(self):
        s = self.slots[self.i % len(self.slots)]
        self.i += 1
        return s


NSL = 4
NS = 128
NT = NPT + NS
SCALE = 0.125
NEG = -30000.0


class _Done(Exception):
    pass


KCUT = -1
KSKIP = []
_cpn = [0]


def build_program():
    nc = bass.Bass("TRN2", target_bir_lowering=False)
    stack = contextlib.ExitStack()
    with stack:
        try:
            _build_body(nc, stack)
        except _Done:
            pass
    return nc


def _build_body(nc, stack):
    if True:
        P = Prog(nc, stack)

        def cp(name):
            _cpn[0] += 1
            if _cpn[0] == KCUT:
                print("CUT at", name)
                P.finish(out_bufs)
                P.build()
                raise _Done()

        def dram_in(name, shape, dt=F32):
            return nc.dram_tensor(name, list(shape), dt, kind="ExternalInput")

        def dram_out(name, shape, dt=F32):
            return nc.dram_tensor(name, list(shape), dt, kind="ExternalOutput")

        def dram_tmp(name, shape, dt=F32):
            return nc.dram_tensor(name, list(shape), dt)

        def sb(name, shape, dt):
            t = stack.enter_context(nc.sbuf_tensor(name, list(shape), dt))
            return t, Buf(name)

        def ps(name, shape=(128, 512), dt=F32):
            t = stack.enter_context(nc.psum_tensor(name, list(shape), dt))
            b = Buf(name)
            b.psum = True
            return t, b

        xT = dram_in("xT", [128, KC, NT])
        w1 = [dram_in("w1_%d" % l, [FC, 128, KC, 128]) for l in range(2)]
        w3 = [dram_in("w3_%d" % l, [FC, 128, KC, 128]) for l in range(2)]
        w2 = [dram_in("w2_%d" % l, [KC, 128, FC, 128]) for l in range(2)]
        win_qk = dram_in("win_qk", [16, 128, KC, 64])
        win_v = dram_in("win_v", [128, KC, 512])
        win_f = dram_in("win_f", [128, KC, 8])
        win_c = dram_in("win_c", [12, 128, KC, 128])
        wo_a = dram_in("wo_a", [KC, 64, 8, 128])
        wo_c = dram_in("wo_c", [KC, 128, 4, 128])
        gains = dram_in("gains", [128, 4, KC])
        bfn = dram_in("bfn", [8, 1])
        gao_d = dram_in("gao", [64, 8])
        gco_d = dram_in("gco", [128, 4])
        cw_d = dram_in("convw", [128, 4, 3])
        selj_d = dram_in("selj", [128, 4])
        hsel_d = dram_in("hsel", [128, NSL, 16])
        masks_d = dram_in("masks", [128, 16, 512])
        ident_d = dram_in("ident", [128, 128])
        sel65_d = dram_in("sel65", [65, 64])
        sconv_d = dram_in("sconvT", [128, 4, 64])

        I32 = mybir.dt.int32
        pgid_d = dram_in("pgid", [128, 5])
        tmask_d = dram_in("tmask", [128, 4])
        hmask_d = dram_in("hmask", [128, 2])
        dmask_d = dram_in("dmask", [128, 8])
        pgrow_d = dram_in("pgrow", [128, 640])
        utri_d = dram_in("utri", [128, 128])
        id32_d = dram_in("id32", [32, 32])
        ptT_d = dram_in("ptT", [128, 32], I32)
        ptflat_d = dram_in("ptflat", [1, 4096], I32)
        lfg_d = dram_in("lfg", [128, 5, 1024])
        lfk_d = dram_in("lfk", [128, 640, 8])
        kpT_d = dram_in("kpT", [160, 128, 4, 4, 128])
        vp_d = dram_in("vp", [160, 128, 4, 512])
        win_tm = dram_in("win_tm", [4, 128, KC, 256])
        tbp_in = dram_tmp("tbp_in", [128, 256]); tbp_inb = Buf("tbp_in")
        tbp_out = dram_tmp("tbp_out", [1024, 256]); tbp_outb = Buf("tbp_out")
        od_d = dram_tmp("od_d", [640, 32, 64]); od_db = Buf("od_d")
        ld_d = dram_tmp("ld_d", [640, 32]); ld_db = Buf("ld_d")
        part_in = dram_tmp("part_in", [32, 2080]); part_inb = Buf("part_in")
        part_out = dram_tmp("part_out", [256, 2080]); part_outb = Buf("part_out")
        qs_d = dram_tmp("qs_d", [128, 512]); qs_db = Buf("qs_d")
        ks_d = dram_tmp("ks_d", [128, 512]); ks_db = Buf("ks_d")

        kT_o = dram_out("kT_o", [8, 64, NT])
        v_o = dram_out("v_o", [NT, 512])
        lfT_o = dram_out("lfT_o", [8, NT])
        ucT_o = dram_out("ucT_o", [128, 4, NT])
        yT_o = dram_out("yT_o", [128, KC, NT])
        kT_ob, v_ob, lfT_ob, ucT_ob, yT_ob = [Buf(n) for n in ("kT_o", "v_o", "lfT_o", "ucT_o", "yT_o")]
        out_bufs = [kT_ob, v_ob, lfT_ob, ucT_ob, yT_ob]
        DBG = 'dbg' in KSKIP
        if DBG:
            dbg_acc = dram_out("dbg_acc", [32, 2080]); dbg_accb = Buf("dbg_acc")
            dbg_o = dram_out("dbg_o", [32, 2080]); dbg_ob = Buf("dbg_o")
            dbg_w = dram_out("dbg_w", [128, 256]); dbg_wb = Buf("dbg_w")
            dbg_ld = dram_out("dbg_ld", [640, 32]); dbg_ldb = Buf("dbg_ld")
            dbg_od = dram_out("dbg_od", [640, 2048]); dbg_odb = Buf("dbg_od")
            dbg_wpg = dram_out("dbg_wpg", [128, 40]); dbg_wpgb = Buf("dbg_wpg")
            dbg_own = dram_out("dbg_own", [128, 160]); dbg_ownb = Buf("dbg_own")
            out_bufs += [dbg_accb, dbg_ob, dbg_wb, dbg_ldb, dbg_odb, dbg_wpgb, dbg_ownb]

        xres_d = dram_tmp("xres_d", [128, KC, NT]); xres_b = Buf("xres_d")
        bg_d = dram_tmp("bg_d", [128, 4, NT]); bg_db = Buf("bg_d")
        oat_d = dram_tmp("oat_d", [8, 64, NT]); oat_b = Buf("oat_d")
        agk_in = [dram_tmp("agk_in%d" % i, [128, NPT // 2]) for i in range(4)]; agk_inb = [Buf("agk_in")] * 4
        agk_out = [dram_tmp("agk_out%d" % i, [512, NPT // 2]) for i in range(4)]; agk_outb = [Buf("agk_out")] * 4
        agv_in = [dram_tmp("agv_in%d" % i, [128, 8 * 65]) for i in range(8)]; agv_inb = [Buf("agv_in")] * 8
        agv_out = [dram_tmp("agv_out%d" % i, [512, 8 * 65]) for i in range(8)]; agv_outb = [Buf("agv_out")] * 8
        agf_in = dram_tmp("agf_in", [10, NPT]); agf_inb = Buf("agf_in")
        agf_out = dram_tmp("agf_out", [40, NPT]); agf_outb = Buf("agf_out")
        cs_d = dram_tmp("cs_d", [3, 8, 8192], BF16); cs_db = Buf("cs_d")
        ncs_d = dram_tmp("ncs_d", [3, 8, 8192], BF16); ncs_db = Buf("ncs_d")

        ones_t, ones_b = sb("ones", [128, 128], BF16)
        ident_t, ident_b = sb("ident_sb", [128, 128], BF16)
        sel65_t, sel65_b = sb("sel65_sb", [65, 64], F32)
        gains_t, gains_b = sb("gains_sb", [128, 4, KC], F32)
        gao_t, gao_b = sb("gao_sb", [64, 8], F32)
        gco_t, gco_b = sb("gco_sb", [128, 4], F32)
        cw_t, cw_b = sb("cw_sb", [128, 4, 3], F32)
        selj_t, selj_b = sb("selj_sb", [128, 4], F32)
        hsel_t, hsel_b = sb("hsel_sb", [128, NSL, 16], F32)
        nbf_t, nbf_b = sb("nbf", [8, 1], F32)
        eps_t, eps_b = sb("eps", [128, 1], F32)
        one8_t, one8_b = sb("one8", [8, 1], F32)
        x_ring = Ring([sb("x%d" % i, [128, KC, 512], F32) for i in range(2)])
        rstd_t, rstd_b = sb("rstd", [128, 512], F32)
        s_ring = Ring([sb("s%d" % i, [128, 512], F32) for i in range(2)])
        ev_ring = Ring([sb("ev%d" % i, [128, 512], F32) for i in range(3)])
        wv_t, wv_b = sb("wv", [128, KC, 512], BF16)
        wf_t, wf_b = sb("wf", [128, KC, 8], BF16)
        lf_ring = Ring([sb("lf%d" % i, [8, 512], F32) for i in range(2)])
        e_t, e_b = sb("e_sb", [8, 512], F32)
        lfst_t, lfst_b = sb("lfst", [8, NPT], F32)
        vb_ring = Ring([sb("vb%d" % i, [128, 8, 65], BF16) for i in range(2)])
        kst_ring = Ring([sb("kst%d" % i, [64, 512], BF16) for i in range(2)])
        pT_ring = Ring([sb("pT%d" % i, [128, 512], BF16) for i in range(3)])
        osb_t, osb_b = sb("osb", [65, 512], F32)
        rl_t, rl_b = sb("rl", [64, 512], F32)
        tails_t, tails_b = sb("tails", [128, 4, 32], F32)
        qtok_t, qtok_b = sb("qtok", [128, 512], BF16)
        a1, _ = sb("arena1", [128, 19456], BF16)
        a2, _ = sb("arena2", [128, 11776], BF16)
        qreg, _ = sb("qreg", [128, 16384], BF16)
        mreg, _ = sb("mreg", [128, 8192], BF16)

        sq_t = a1[:, 0:4096].rearrange("p (c n) -> p c n", c=KC); sq_b = Buf("sq")
        xn_t = a1[:, 4096:8192].rearrange("p (c n) -> p c n", c=KC); xn_b = Buf("xn")
        h1_t = a1[:, 8192:19456].rearrange("p (c n) -> p c n", c=FC)
        h1_b = [Buf("h1_%d" % j) for j in range(FC)]
        w1_ring = Ring([(a2[:, i * 1024:(i + 1) * 1024].rearrange("p (c n) -> p c n", c=KC), Buf("w1s%d" % i)) for i in range(3)])
        w3_ring = Ring([(a2[:, 3072 + i * 1024:3072 + (i + 1) * 1024].rearrange("p (c n) -> p c n", c=KC), Buf("w3s%d" % i)) for i in range(3)])
        w2_ring = Ring([(a2[:, 6144 + i * 2816:6144 + (i + 1) * 2816].rearrange("p (c n) -> p c n", c=FC), Buf("w2s%d" % i)) for i in range(2)])
        kp_slots = [(a1[:, i * 8192:(i + 1) * 8192], Buf("kp%d" % i)) for i in range(2)]
        vp_slots = [(a2[:, i * 4160:(i + 1) * 4160].rearrange("p (b e) -> p b e", e=65), Buf("vp%d" % i)) for i in range(2)]
        qT_t = qreg[:, :].rearrange("p (h n) -> p h n", h=8); qT_b = Buf("qT")
        mask_t = mreg[:, :].rearrange("p (m n) -> p m n", m=16); mask_b = Buf("mask")
        lfc_t = a1[0:8, 0:4096].bitcast(F32); lfc_b = Buf("lfc")
        cgc_t = a1[0:8, 4096:8192].bitcast(F32); cgc_b = Buf("cgc")
        b16_t = [a1[0:8, 8192 + i * 2048:8192 + (i + 1) * 2048] for i in range(3)]
        b16_b = [Buf("b16_%d" % i) for i in range(3)]
        t4_t = a1[:, 8192:16384].rearrange("p (s j n) -> p s j n", s=4, j=4); t4_b = Buf("t4")
        oc_t = qreg[0:64, 0:8192].bitcast(F32).rearrange("p (h n) -> p h n", h=8); oc_b = Buf("oc")
        bgt_t = qreg[:, 8192:12288].bitcast(F32).rearrange("p (c n) -> p c n", c=4); bgt_b = Buf("bgt")
        uct_t = mreg[:, 0:4608].bitcast(F32).rearrange("p (c n) -> p c n", c=4); uct_b = Buf("uct")
        mixa_t = a1[0:64, 8192:12288].rearrange("p (h n) -> p h n", h=8); mixa_b = Buf("mixa")
        mixc_t = a1[:, 12288:14336].rearrange("p (c n) -> p c n", c=4); mixc_b = Buf("mixc")

        psA = Ring([ps("psA%d" % i) for i in range(2)])
        psB = Ring([ps("psB%d" % i) for i in range(2)])
        psY = Ring([ps("psY%d" % i) for i in range(2)])
        psS_t, psS_b = ps("psS")
        psM_t, psM_b = ps("psM")
        psQK = Ring(psA.slots + psB.slots)

        def pe_fence(bufs):
            tok = P.op("tensor", lambda e: e.matmul(psS_t[0:32, 504:512], lhsT=ident_t[:, 0:32], rhs=ident_t[:, 0:8],
                                                    start=True, stop=True), reads=[ident_b], writes=[])
            for b in bufs:
                b.w = tok

        P.op("vector", lambda e: e.memset(ones_t[:], 1.0), writes=[ones_b])
        P.op("vector", lambda e: e.memset(eps_t[:], EPS), writes=[eps_b])
        P.op("vector", lambda e: e.memset(one8_t[:], 1.0), writes=[one8_b])
        cg = []
        for (t_, b_, d_) in ((gains_t, gains_b, gains), (nbf_t, nbf_b, bfn), (gao_t, gao_b, gao_d), (gco_t, gco_b, gco_d),
                             (cw_t, cw_b, cw_d), (selj_t, selj_b, selj_d), (hsel_t, hsel_b, hsel_d), (sel65_t, sel65_b, sel65_d)):
            cg.append(("sync", (lambda e, t_=t_, d_=d_: e.dma_start(out=t_[:], in_=d_.ap())), b_))
        cg.append(("gpsimd", (lambda e: e.dma_start(out=wv_t[:], in_=win_v.ap(), max_dma_last_dim=4096)), wv_b))
        cg.append(("gpsimd", (lambda e: e.dma_start(out=wf_t[:], in_=win_f.ap())), wf_b))
        cg.append(("gpsimd", (lambda e: e.dma_start(out=ident_t[:], in_=ident_d.ap())), ident_b))
        cg.append(("gpsimd", (lambda e: e.dma_start(out=mask_t[:], in_=masks_d.ap(), max_dma_last_dim=4096)), mask_b))
        P.const_group(cg)
        P.op("vector", lambda e: e.tensor_scalar(out=nbf_t[:], in0=nbf_t[:], scalar1=-1.0, scalar2=None,
                                                 op0=ALU.mult), reads=[nbf_b], writes=[nbf_b])
        for (vb, vbb) in vb_ring.slots:
            P.op("vector", lambda e, vb=vb: e.memset(vb[:], 1.0), writes=[vbb])
        P.op("vector", lambda e: e.memset(qT_t[64:70, :, :], 1.0), writes=[qT_b])

        def rmsnorm_stats(sq_ap_fn, nchunks, kparts, src_bufs, N, dim):
            for c in range(nchunks):
                P.op("tensor", lambda e, c=c: e.matmul(psS_t[:, 0:N], lhsT=ones_t[0:kparts, :], rhs=sq_ap_fn(c),
                                                       start=(c == 0), stop=(c == nchunks - 1)),
                     reads=[ones_b] + src_bufs, writes=[psS_b])
            P.op("scalar", lambda e: e.activation(out=rstd_t[:, 0:N], in_=psS_t[:, 0:N], func=AF.Sqrt,
                                                  bias=eps_t[:, 0:1], scale=1.0 / dim),
                 reads=[psS_b, eps_b], writes=[rstd_b])
            P.op("vector", lambda e: e.reciprocal(out=rstd_t[:, 0:N], in_=rstd_t[:, 0:N]),
                 reads=[rstd_b], writes=[rstd_b])

        def rmsnorm(x_t, x_b, gi, N):
            P.op("scalar", lambda e: e.activation(out=sq_t[:, :, 0:N], in_=x_t[:, :, 0:N], func=AF.Square),
                 reads=[x_b], writes=[sq_b])
            rmsnorm_stats(lambda c: sq_t[:, c, 0:N], KC, 128, [sq_b], N, D)
            for c in range(KC):
                P.op("vector", lambda e, c=c: e.scalar_tensor_tensor(
                    out=xn_t[:, c, 0:N], in0=x_t[:, c, 0:N], scalar=gains_t[:, gi, c:c + 1],
                    in1=rstd_t[:, 0:N], op0=ALU.mult, op1=ALU.mult),
                     reads=[x_b, gains_b, rstd_b], writes=[xn_b])

        def ffn(l, x_t, x_b, N):
            for j in range(FC):
                (w1s, w1sb) = w1_ring.next()
                (w3s, w3sb) = w3_ring.next()
                P.dma("gpsimd", lambda e, j=j, w1s=w1s: e.dma_start(out=w1s, in_=w1[l][j], max_dma_last_dim=4096), w1sb)
                P.dma("gpsimd", lambda e, j=j, w3s=w3s: e.dma_start(out=w3s, in_=w3[l][j], max_dma_last_dim=4096), w3sb)
                (pa, pab) = psA.next()
                (pb, pbb) = psB.next()
                for c in range(KC):
                    P.op("tensor", lambda e, c=c, pa=pa, w1s=w1s: e.matmul(
                        pa[:, 0:N], lhsT=w1s[:, c, :], rhs=xn_t[:, c, 0:N], start=(c == 0), stop=(c == KC - 1)),
                         reads=[w1sb, xn_b], writes=[pab])
                for c in range(KC):
                    P.op("tensor", lambda e, c=c, pb=pb, w3s=w3s: e.matmul(
                        pb[:, 0:N], lhsT=w3s[:, c, :], rhs=xn_t[:, c, 0:N], start=(c == 0), stop=(c == KC - 1)),
                         reads=[w3sb, xn_b], writes=[pbb])
                (s_t, s_b) = s_ring.next()
                P.op("scalar", lambda e, pa=pa, s_t=s_t: e.activation(out=s_t[:, 0:N], in_=pa[:, 0:N], func=AF.Silu),
                     reads=[pab], writes=[s_b])
                P.op("vector", lambda e, j=j, pb=pb, s_t=s_t: e.tensor_tensor(
                    out=h1_t[:, j, 0:N], in0=s_t[:, 0:N], in1=pb[:, 0:N], op=ALU.mult),
                     reads=[s_b, pbb], writes=[h1_b[j]])
            for m in range(KC):
                (w2s, w2sb) = w2_ring.next()
                P.dma("gpsimd", lambda e, m=m, w2s=w2s: e.dma_start(out=w2s, in_=w2[l][m], max_dma_last_dim=4096), w2sb)
                (py, pyb) = psY.next()
                for j in range(FC):
                    P.op("tensor", lambda e, j=j, py=py, w2s=w2s: e.matmul(
                        py[:, 0:N], lhsT=w2s[:, j, :], rhs=h1_t[:, j, 0:N], start=(j == 0), stop=(j == FC - 1)),
                         reads=[w2sb, h1_b[j]], writes=[pyb])
                P.op("vector", lambda e, m=m, py=py: e.scalar_tensor_tensor(
                    out=x_t[:, m, 0:N], in0=py[:, 0:N], scalar=0.5, in1=x_t[:, m, 0:N], op0=ALU.mult, op1=ALU.add),
                     reads=[pyb, x_b], writes=[x_b])

        def proj_fm(src, N, mcols=128, kparts=128):
            (ws, wsb) = w1_ring.next()
            P.dma("gpsimd", lambda e: e.dma_start(out=ws[:, :, 0:mcols], in_=src, max_dma_last_dim=4096), wsb)
            (pa, pab) = psA.next()
            for c in range(KC):
                P.op("tensor", lambda e, c=c: e.matmul(
                    pa[0:mcols, 0:N], lhsT=ws[:, c, 0:mcols], rhs=xn_t[:, c, 0:N], start=(c == 0), stop=(c == KC - 1)),
                     reads=[wsb, xn_b], writes=[pab])
            return pa, pab

        def phase_a(t0, N, slot):
            is_prompt = slot is not None
            (x_t, x_b) = x_ring.next()
            P.dma("sync", lambda e: e.dma_start(out=x_t[:, :, 0:N], in_=xT[:, :, t0:t0 + N]), x_b)
            cp('load')
            rmsnorm(x_t, x_b, 0, N)
            cp('norm1')
            ffn(0, x_t, x_b, N)
            P.dma("sync", lambda e: e.dma_start(out=xres_d[:, :, t0:t0 + N], in_=x_t[:, :, 0:N]), xres_b, reads=[x_b], src=x_b)
            rmsnorm(x_t, x_b, 1, N)
            cp('ffn_spill_norm2')
            if is_prompt:
                for h in range(8):
                    pa, pab = proj_fm(win_qk[h], N, mcols=64)
                    P.op("scalar", lambda e, h=h, pa=pa: e.activation(out=qT_t[0:64, h, t0:t0 + N], in_=pa[0:64, 0:N],
                                                                      func=AF.Identity, scale=SCALE),
                         reads=[pab], writes=[qT_b])
            cp('q')
            for h in range(8):
                pa, pab = proj_fm(win_qk[8 + h], N, mcols=64)
                (ev, evb) = ev_ring.next()
                P.op("vector", lambda e, pa=pa, ev=ev: e.tensor_copy(out=ev[0:64, 0:N], in_=pa[0:64, 0:N]),
                     reads=[pab], writes=[evb])
                P.dma("sync", lambda e, h=h, ev=ev: e.dma_start(out=kT_o[h, :, t0:t0 + N], in_=ev[0:64, 0:N]),
                      kT_ob, reads=[evb], src=evb)
                if is_prompt and 'agk' not in KSKIP:
                    (ks, ksb) = kst_ring.next()
                    P.op("scalar", lambda e, ev=ev, ks=ks: e.activation(out=ks[:, 0:N], in_=ev[0:64, 0:N], func=AF.Identity),
                         reads=[evb], writes=[ksb])
                    P.dma("sync", lambda e, h=h, ks=ks: e.dma_start(
                        out=agk_in[h // 2].bitcast(BF16)[(h % 2) * 64:(h % 2 + 1) * 64, t0:t0 + N], in_=ks[:, 0:N]),
                          agk_inb[h // 2], reads=[ksb], src=ksb)
            cp('k')
            for tb in range(N // 128):
                (py, pyb) = psY.next()
                for c in range(KC):
                    P.op("tensor", lambda e, c=c, py=py, tb=tb: e.matmul(
                        py[:, :], lhsT=xn_t[:, c, tb * 128:(tb + 1) * 128], rhs=wv_t[:, c, :],
                        start=(c == 0), stop=(c == KC - 1)),
                         reads=[wv_b, xn_b], writes=[pyb])
                (ev, evb) = ev_ring.next()
                P.op("vector", lambda e, py=py, ev=ev: e.tensor_copy(out=ev[:, :], in_=py[:, :]),
                     reads=[pyb], writes=[evb])
                P.dma("sync", lambda e, ev=ev, tb=tb: e.dma_start(
                    out=v_o[t0 + tb * 128:t0 + (tb + 1) * 128, :], in_=ev[:, :]), v_ob, reads=[evb], src=evb)
                if is_prompt:
                    (vb, vbb) = vb_ring.next()
                    P.op("scalar", lambda e, ev=ev, vb=vb: e.activation(
                        out=vb[:, :, 0:64], in_=ev[:, :].rearrange("p (h d) -> p h d", h=8), func=AF.Identity),
                         reads=[evb], writes=[vbb])
                    blk = slot * 4 + tb
                    for h in range(8):
                        P.dma("sync", lambda e, vb=vb, blk=blk, h=h: e.dma_start(
                            out=agv_in[h].bitcast(BF16).ap().rearrange("p (b e) -> p b e", e=65)[:, blk, :], in_=vb[:, h, :]),
                              agv_inb[h], reads=[vbb], src=vbb)
            cp('v')
            if not is_prompt:
                for qi in range(4):
                    (ws, wsb) = w2_ring.next()
                    wsv = ws[:, :, :].rearrange("p c n -> p (c n)")[:, 0:2048].rearrange("p (c n) -> p c n", c=KC)
                    P.dma("gpsimd", lambda e, qi=qi, wsv=wsv: e.dma_start(out=wsv, in_=win_tm[qi], max_dma_last_dim=4096), wsb)
                    (py, pyb) = psY.next()
                    for c in range(KC):
                        P.op("tensor", lambda e, c=c, py=py, wsv=wsv: e.matmul(
                            py[:, 0:256], lhsT=xn_t[:, c, 0:128], rhs=wsv[:, c, :], start=(c == 0), stop=(c == KC - 1)),
                             reads=[wsb, xn_b], writes=[pyb])
                    (ev, evb) = ev_ring.next()
                    half = qi % 2
                    if qi < 2:
                        P.op("scalar", lambda e, py=py, ev=ev: e.activation(out=ev[:, 0:256], in_=py[:, 0:256], func=AF.Identity, scale=SCALE),
                             reads=[pyb], writes=[evb])
                        P.op("vector", lambda e, ev=ev, half=half: e.tensor_copy(out=qtok_t[:, half * 256:(half + 1) * 256], in_=ev[:, 0:256]),
                             reads=[evb], writes=[qtok_b])
                        P.dma("sync", lambda e, ev=ev, half=half: e.dma_start(out=qs_d[:, half * 256:(half + 1) * 256], in_=ev[:, 0:256]),
                              qs_db, reads=[evb], src=evb)
                    else:
                        P.op("vector", lambda e, py=py, ev=ev: e.tensor_copy(out=ev[:, 0:256], in_=py[:, 0:256]), reads=[pyb], writes=[evb])
                        P.dma("sync", lambda e, ev=ev, half=half: e.dma_start(out=ks_d[:, half * 256:(half + 1) * 256], in_=ev[:, 0:256]),
                              ks_db, reads=[evb], src=evb)
            for c in range(KC):
                P.op("tensor", lambda e, c=c: e.matmul(psM_t[0:8, 0:N], lhsT=wf_t[:, c, :], rhs=xn_t[:, c, 0:N],
                                                       start=(c == 0), stop=(c == KC - 1)),
                     reads=[wf_b, xn_b], writes=[psM_b])
            P.op("scalar", lambda e: e.activation(out=e_t[:, 0:N], in_=psM_t[0:8, 0:N], func=AF.Exp,
                                                  bias=nbf_t[:, 0:1], scale=-1.0),
                 reads=[psM_b, nbf_b], writes=[e_b])
            P.op("scalar", lambda e: e.activation(out=e_t[:, 0:N], in_=e_t[:, 0:N], func=AF.Ln, bias=1.0, scale=1.0),
                 reads=[e_b], writes=[e_b])
            (lf, lfb) = lf_ring.next()
            P.op("vector", lambda e: e.tensor_scalar(out=lf[:, 0:N], in0=e_t[:, 0:N], scalar1=-1.0,
                                                     scalar2=None, op0=ALU.mult),
                 reads=[e_b], writes=[lfb])
            P.dma("sync", lambda e: e.dma_start(out=lfT_o[:, t0:t0 + N], in_=lf[:, 0:N]), lfT_ob, reads=[lfb], src=lfb)
            if is_prompt:
                P.op("vector", lambda e: e.tensor_copy(out=lfst_t[:, t0:t0 + N], in_=lf[:, 0:N]),
                     reads=[lfb], writes=[lfst_b])
            cp('fgate')
            for i in range(4):
                pg, pgb = proj_fm(win_c[i], N)
                (ev, evb) = ev_ring.next()
                P.op("scalar", lambda e, pg=pg, ev=ev: e.activation(out=ev[:, 0:N], in_=pg[:, 0:N], func=AF.Identity),
                     reads=[pgb], writes=[evb])
                P.dma("sync", lambda e, i=i, ev=ev: e.dma_start(out=bg_d[:, i, t0:t0 + N], in_=ev[:, 0:N]), bg_db, reads=[evb], src=evb)
                pc, pcb = proj_fm(win_c[4 + i], N)
                (cg, cgb) = s_ring.next()
                P.op("scalar", lambda e, pc=pc, cg=cg: e.activation(out=cg[:, 0:N], in_=pc[:, 0:N], func=AF.Identity),
                     reads=[pcb], writes=[cgb])
                pu, pub = proj_fm(win_c[8 + i], N)
                (ev, evb) = ev_ring.next()
                P.op("vector", lambda e, pu=pu, cg=cg, ev=ev: e.tensor_tensor(
                    out=ev[:, 0:N], in0=cg[:, 0:N], in1=pu[:, 0:N], op=ALU.mult),
                     reads=[cgb, pub], writes=[evb])
                P.dma("sync", lambda e, i=i, ev=ev: e.dma_start(out=ucT_o[:, i, t0:t0 + N], in_=ev[:, 0:N]),
                      ucT_ob, reads=[evb], src=evb)
                if is_prompt:
                    P.dma("sync", lambda e, i=i, ev=ev: e.dma_start(
                        out=agf_in[8:10, :].rearrange("r n -> (r n)").rearrange("(i s p t) -> p i s t", i=4, s=4, t=2)[:, i, slot, :],
                        in_=ev[:, N - 2:N]), agf_inb, reads=[evb], src=evb)

        cp('consts')
        for s in range(NSL):
            phase_a(s * 512, 512, s)
            cp('tile%d' % s)
        phase_a(NPT, NS, None)

        if STAGE < 2:
            P.finish(out_bufs); P.build(); return nc
        P.dma("sync", lambda e: e.dma_start(out=agf_in[0:8, :], in_=lfst_t[:, :]), agf_inb, reads=[lfst_b])
        RG = [[0, 1, 2, 3], [4, 5, 6, 7]]
        if 'ccf' not in KSKIP:
            P.cc(lambda e: e.collective_compute("AllGather", ALU.bypass, replica_groups=RG,
                                                ins=[agf_in.ap().opt()], outs=[agf_out.ap().opt()]), agf_outb, reads=[agf_inb])
        for i in range(4):
            P.cc(lambda e, i=i: e.collective_compute("AllGather", ALU.bypass, replica_groups=RG,
                                                     ins=[agk_in[i].ap().opt()], outs=[agk_out[i].ap().opt()]),
                 agk_outb[i], reads=[agk_inb[i]])
        for h in range(8):
            P.cc(lambda e, h=h: e.collective_compute("AllGather", ALU.bypass, replica_groups=RG,
                                                     ins=[agv_in[h].ap().opt()], outs=[agv_out[h].ap().opt()]),
                 agv_outb[h], reads=[agv_inb[h]])
        if STAGE < 3:
            P.finish(out_bufs); P.build(); return nc
        P.barrier([sq_b, xn_b] + h1_b)
        agf_v = agf_out.ap().rearrange("(r q) n -> r q n", r=4)
        prev = None
        for ch in range(4):
            for r in range(4):
                P.dma("sync", lambda e, ch=ch, r=r: e.dma_start(
                    out=lfc_t[:, r * 512:(r + 1) * 512], in_=agf_v[r, 0:8, ch * 512:(ch + 1) * 512]),
                      lfc_b, reads=[agf_outb])
            P.op("vector", lambda e, ch=ch: e.tensor_tensor_scan(
                out=cgc_t[:, :], data0=one8_t[:, 0:1].to_broadcast([8, 2048]), data1=lfc_t[:, :],
                initial=(0.0 if ch == 0 else lf_ring.slots[0][0][:, 0:1]), op0=ALU.mult, op1=ALU.add),
                 reads=[lfc_b, one8_b, lf_ring.slots[0][1]], writes=[cgc_b])
            P.op("vector", lambda e: e.tensor_copy(out=lf_ring.slots[0][0][:, 0:1], in_=cgc_t[:, 2047:2048]),
                 reads=[cgc_b], writes=[lf_ring.slots[0][1]])
            P.op("vector", lambda e: e.tensor_copy(out=b16_t[0], in_=cgc_t[:, :]), reads=[cgc_b], writes=[b16_b[0]])
            P.op("vector", lambda e: e.tensor_tensor(out=cgc_t[:, :], in0=cgc_t[:, :], in1=b16_t[0], op=ALU.subtract),
                 reads=[cgc_b, b16_b[0]], writes=[cgc_b])
            P.op("vector", lambda e: e.tensor_copy(out=b16_t[1], in_=cgc_t[:, :]), reads=[cgc_b], writes=[b16_b[1]])
            P.op("vector", lambda e: e.tensor_tensor(out=cgc_t[:, :], in0=cgc_t[:, :], in1=b16_t[1], op=ALU.subtract),
                 reads=[cgc_b, b16_b[1]], writes=[cgc_b])
            P.op("vector", lambda e: e.tensor_copy(out=b16_t[2], in_=cgc_t[:, :]), reads=[cgc_b], writes=[b16_b[2]])
            for i in range(3):
                P.dma("sync", lambda e, i=i, ch=ch: e.dma_start(out=cs_d[i, :, ch * 2048:(ch + 1) * 2048], in_=b16_t[i]),
                      cs_db, reads=[b16_b[i]])
            for i in range(3):
                P.op("vector", lambda e, i=i: e.tensor_scalar(out=b16_t[i], in0=b16_t[i], scalar1=-1.0, scalar2=None, op0=ALU.mult),
                     reads=[b16_b[i]], writes=[b16_b[i]])
                P.dma("sync", lambda e, i=i, ch=ch: e.dma_start(out=ncs_d[i, :, ch * 2048:(ch + 1) * 2048], in_=b16_t[i]),
                      ncs_db, reads=[b16_b[i]])
        P.barrier(b16_b + [lfc_b, cgc_b])
        for h in range(8):
            P.dma("sync", lambda e, h=h: e.dma_start(
                out=t4_t[64:67, :, :, :], in_=cs_d[:, h, :].rearrange("i (s j n) -> i s j n", s=4, j=4)), t4_b, reads=[cs_db])
            for j in range(4):
                if j == 0:
                    P.op("vector", lambda e, h=h: e.tensor_scalar(
                        out=qT_t[64:67, h, :].rearrange("p (s n) -> p s n", s=4), in0=t4_t[64:67, :, 0, :],
                        scalar1=selj_t[64:67, 0:1], scalar2=None, op0=ALU.mult),
                         reads=[t4_b, selj_b], writes=[qT_b])
                else:
                    P.op("vector", lambda e, h=h, j=j: e.scalar_tensor_tensor(
                        out=qT_t[64:67, h, :].rearrange("p (s n) -> p s n", s=4), in0=t4_t[64:67, :, j, :],
                        scalar=selj_t[64:67, j:j + 1], in1=qT_t[64:67, h, :].rearrange("p (s n) -> p s n", s=4),
                        op0=ALU.mult, op1=ALU.add),
                         reads=[t4_b, selj_b, qT_b], writes=[qT_b])
        for r in range(4):
            P.dma("sync", lambda e, r=r: e.dma_start(
                out=tails_t[:, r, :].rearrange("p (q t) -> p q t", t=2),
                in_=agf_v[r, 8:10, :].rearrange("q n -> (q n)").rearrange("(q p t) -> p q t", q=16, t=2)),
                  tails_b, reads=[agf_outb])

        if STAGE < 4:
            P.finish(out_bufs); P.build(); return nc
        P.barrier([t4_b] + [b for (_, b) in w1_ring.slots + w3_ring.slots + w2_ring.slots])
        for (kp, kpb) in kp_slots:
            P.op("vector", lambda e, kp=kp: e.memset(kp[64:70, :], 1.0), writes=[kpb])
        agk_v = [t.bitcast(BF16).ap().rearrange("(r h d) n -> r h d n", r=4, h=2) for t in agk_out]
        agv_v = [t.bitcast(BF16).ap().rearrange("(r p) (s k e) -> r p s k e", r=4, s=4, k=4) for t in agv_out]
        for h in range(8):
            (kp, kpb) = kp_slots[h % 2]
            (vp, vpb) = vp_slots[h % 2]
            for r in range(4):
                P.dma("sync", lambda e, h=h, r=r, kp=kp: e.dma_start(
                    out=kp[0:64, :].rearrange("p (s r n) -> p s r n", s=4, r=4)[:, :, r, :],
                    in_=agk_v[h // 2][r, h % 2].rearrange("d (s n) -> d s n", s=4)), kpb, reads=[agk_outb[h // 2]])
                P.dma("sync", lambda e, h=h, r=r, vp=vp: e.dma_start(
                    out=vp.rearrange("p (s r k) e -> p s r k e", s=4, r=4)[:, :, r, :, :],
                    in_=agv_v[h][r]), vpb, reads=[agv_outb[h]])
            P.dma("sync", lambda e, h=h, kp=kp: e.dma_start(out=kp[67:70, :], in_=ncs_d[:, h, :]), kpb, reads=[ncs_db])
            for s in range(NSL):
                nblk = (4 * s + 4) * 4
                (po, pob) = psY.next()
                pend = None
                for blk in range(nblk):
                    (pq, pqb) = psQK.next()
                    diag = blk >= 16 * s
                    P.op("tensor", lambda e, blk=blk, pq=pq, kp=kp, h=h, s=s, diag=diag: e.matmul(
                        pq[:, :], lhsT=kp[0:70, blk * 128:(blk + 1) * 128], rhs=qT_t[0:70, h, s * 512:(s + 1) * 512],
                        start=True, stop=(not diag)), reads=[kpb, qT_b], writes=[pqb])
                    if diag:
                        mi = blk - 16 * s
                        P.op("tensor", lambda e, pq=pq, mi=mi: e.matmul(
                            pq[:, :], lhsT=ident_t[:, :], rhs=mask_t[:, mi, :], start=False, stop=True),
                             reads=[ident_b, mask_b], writes=[pqb])
                    (pt, ptb) = pT_ring.next()
                    P.op("scalar", lambda e, pq=pq, pt=pt: e.activation(out=pt[:, :], in_=pq[:, :], func=AF.Exp),
                         reads=[pqb], writes=[ptb])
                    P.op("tensor", lambda e, blk=blk, po=po, vp=vp, pt=pt, nblk=nblk: e.matmul(
                        po[0:65, :], lhsT=vp[:, blk, :], rhs=pt[:, :], start=(blk == 0), stop=(blk == nblk - 1)),
                         reads=[vpb, ptb], writes=[pob])
                P.op("vector", lambda e, po=po: e.tensor_copy(out=osb_t[:, :], in_=po[0:65, :]), reads=[pob], writes=[osb_b])
                P.op("tensor", lambda e: e.matmul(psM_t[0:64, :], lhsT=sel65_t[:, :], rhs=osb_t[:, :], start=True, stop=True),
                     reads=[sel65_b, osb_b], writes=[psM_b])
                P.op("vector", lambda e: e.reciprocal(out=rl_t[:, :], in_=psM_t[0:64, :]), reads=[psM_b], writes=[rl_b])
                (ev, evb) = ev_ring.next()
                P.op("vector", lambda e, ev=ev: e.tensor_tensor(out=ev[0:64, :], in0=osb_t[0:64, :], in1=rl_t[:, :], op=ALU.mult),
                     reads=[osb_b, rl_b], writes=[evb])
                P.dma("sync", lambda e, h=h, s=s, ev=ev: e.dma_start(out=oat_d[h, :, s * 512:(s + 1) * 512], in_=ev[0:64, :]),
                      oat_b, reads=[evb], src=evb)

        if STAGE < 5:
            P.finish(out_bufs); P.build(); return nc
        P.barrier([qT_b, mask_b] + [b for (_, b) in kp_slots] + [b for (_, b) in vp_slots])

        def phase_c(t0, N, slot):
            is_prompt = slot is not None
            sh = 1 if is_prompt else 32
            hw = 2 * sh
            (x_t, x_b) = x_ring.next()
            P.dma("sync", lambda e: e.dma_start(out=x_t[:, :, 0:N], in_=xres_d[:, :, t0:t0 + N]), x_b, reads=[xres_b])
            P.dma("sync", lambda e: e.dma_start(out=oc_t[:, :, 0:N], in_=oat_d.ap().rearrange("h d n -> d h n")[:, :, t0:t0 + N]),
                  oc_b, reads=[oat_b])
            P.dma("sync", lambda e: e.dma_start(out=bgt_t[:, :, 0:N], in_=bg_d[:, :, t0:t0 + N]), bgt_b, reads=[bg_db])
            P.dma("sync", lambda e: e.dma_start(out=uct_t[:, :, hw:hw + N], in_=ucT_o[:, :, t0:t0 + N]), uct_b, reads=[ucT_ob])
            if is_prompt:
                tv = tails_t[:, :, :].rearrange("p r (i s t) -> p r i s t", i=4, s=4)
                for q in range(16):
                    r_, s_ = q // 4, q % 4
                    if q == 0:
                        P.op("vector", lambda e: e.tensor_scalar(
                            out=uct_t[:, :, 0:2], in0=tv[:, 0, :, 0, :], scalar1=hsel_t[:, slot, 0:1], scalar2=None, op0=ALU.mult),
                             reads=[tails_b, hsel_b], writes=[uct_b])
                    else:
                        P.op("vector", lambda e, q=q, r_=r_, s_=s_: e.scalar_tensor_tensor(
                            out=uct_t[:, :, 0:2], in0=tv[:, r_, :, s_, :], scalar=hsel_t[:, slot, q:q + 1],
                            in1=uct_t[:, :, 0:2], op0=ALU.mult, op1=ALU.add),
                             reads=[tails_b, hsel_b, uct_b], writes=[uct_b])
            else:
                P.dma("sync", lambda e: e.dma_start(out=uct_t[:, :, 0:hw], in_=sconv_d.ap()), uct_b)
            P.op("scalar", lambda e: e.activation(out=sq_t[0:64, :, 0:N], in_=oc_t[:, :, 0:N], func=AF.Square),
                 reads=[oc_b], writes=[sq_b])
            rmsnorm_stats(lambda c: sq_t[0:64, c, 0:N], 8, 64, [sq_b], N, 512)
            for h in range(8):
                P.op("vector", lambda e, h=h: e.scalar_tensor_tensor(
                    out=mixa_t[:, h, 0:N], in0=oc_t[:, h, 0:N], scalar=gao_t[:, h:h + 1], in1=rstd_t[0:64, 0:N],
                    op0=ALU.mult, op1=ALU.mult), reads=[oc_b, gao_b, rstd_b], writes=[mixa_b])
            for i in range(4):
                (ev, evb) = ev_ring.next()
                P.op("vector", lambda e, i=i, ev=ev: e.tensor_scalar(
                    out=ev[:, 0:N], in0=uct_t[:, i, 0:N], scalar1=cw_t[:, i, 0:1], scalar2=None, op0=ALU.mult),
                     reads=[uct_b, cw_b], writes=[evb])
                for w in (1, 2):
                    P.op("vector", lambda e, i=i, ev=ev, w=w: e.scalar_tensor_tensor(
                        out=ev[:, 0:N], in0=uct_t[:, i, w * sh:w * sh + N], scalar=cw_t[:, i, w:w + 1], in1=ev[:, 0:N],
                        op0=ALU.mult, op1=ALU.add), reads=[uct_b, cw_b, evb], writes=[evb])
                P.op("vector", lambda e, i=i, ev=ev: e.tensor_tensor(
                    out=bgt_t[:, i, 0:N], in0=ev[:, 0:N], in1=bgt_t[:, i, 0:N], op=ALU.mult),
                     reads=[evb, bgt_b], writes=[bgt_b])
            P.op("scalar", lambda e: e.activation(out=sq_t[:, 0:4, 0:N], in_=bgt_t[:, :, 0:N], func=AF.Square),
                 reads=[bgt_b], writes=[sq_b])
            rmsnorm_stats(lambda c: sq_t[:, c, 0:N], 4, 128, [sq_b], N, 512)
            for i in range(4):
                P.op("vector", lambda e, i=i: e.scalar_tensor_tensor(
                    out=mixc_t[:, i, 0:N], in0=bgt_t[:, i, 0:N], scalar=gco_t[:, i:i + 1], in1=rstd_t[:, 0:N],
                    op0=ALU.mult, op1=ALU.mult), reads=[bgt_b, gco_b, rstd_b], writes=[mixc_b])
            for m in range(KC):
                (wa, wab) = w1_ring.next()
                (wc, wcb) = w3_ring.next()
                P.dma("gpsimd", lambda e, m=m, wa=wa: e.dma_start(out=wa[0:64, :, :], in_=wo_a[m], max_dma_last_dim=4096), wab)
                P.dma("gpsimd", lambda e, m=m, wc=wc: e.dma_start(out=wc[:, 0:4, :], in_=wo_c[m], max_dma_last_dim=4096), wcb)
                (py, pyb) = psY.next()
                for h in range(8):
                    P.op("tensor", lambda e, h=h, py=py, wa=wa: e.matmul(
                        py[:, 0:N], lhsT=wa[0:64, h, :], rhs=mixa_t[:, h, 0:N], start=(h == 0), stop=False),
                         reads=[wab, mixa_b], writes=[pyb])
                for i in range(4):
                    P.op("tensor", lambda e, i=i, py=py, wc=wc: e.matmul(
                        py[:, 0:N], lhsT=wc[:, i, :], rhs=mixc_t[:, i, 0:N], start=False, stop=(i == 3)),
                         reads=[wcb, mixc_b], writes=[pyb])
                P.op("vector", lambda e, m=m, py=py: e.tensor_tensor(
                    out=x_t[:, m, 0:N], in0=py[:, 0:N], in1=x_t[:, m, 0:N], op=ALU.add),
                     reads=[pyb, x_b], writes=[x_b])
            rmsnorm(x_t, x_b, 2, N)
            P.barrier([mixa_b, mixc_b])
            if 'noffn2' not in KSKIP:
                ffn(1, x_t, x_b, N)
            P.op("scalar", lambda e: e.activation(out=sq_t[:, :, 0:N], in_=x_t[:, :, 0:N], func=AF.Square),
                 reads=[x_b], writes=[sq_b])
            rmsnorm_stats(lambda c: sq_t[:, c, 0:N], KC, 128, [sq_b], N, D)
            for c in range(KC):
                P.op("vector", lambda e, c=c: e.scalar_tensor_tensor(
                    out=x_t[:, c, 0:N], in0=x_t[:, c, 0:N], scalar=gains_t[:, 3, c:c + 1], in1=rstd_t[:, 0:N],
                    op0=ALU.mult, op1=ALU.mult), reads=[x_b, gains_b, rstd_b], writes=[x_b])
            P.dma("sync", lambda e: e.dma_start(out=yT_o[:, :, t0:t0 + N], in_=x_t[:, :, 0:N]), yT_ob, reads=[x_b], src=x_b)
            P.barrier(h1_b)

        for s in range(NSL):
            phase_c(s * 512, 512, s)
        if STAGE < 6:
            P.finish(out_bufs); P.build(); return nc
        NPG, NCH = 640, 5
        P.barrier_all()
        I32 = mybir.dt.int32

        class Arena:
            def __init__(self, t, n, is_f32=False):
                self.t, self.n, self.off, self.f32 = t, n, 0, is_f32

            def take(self, parts, n, dt):
                if self.f32:
                    nn = n if dt is not BF16 else (n + 1) // 2
                else:
                    nn = n if dt is BF16 else 2 * n
                assert self.off + nn <= self.n, (self.off, nn, self.n)
                ap = self.t[0:parts, self.off:self.off + nn]
                self.off += nn
                native = F32 if self.f32 else BF16
                return ap if dt is native else ap.bitcast(dt)

        A1 = Arena(a1, 19456)
        A2 = Arena(a2, 11776)
        AQ = Arena(qreg, 16384)
        AM = Arena(mreg, 8192)
        AX0 = Arena(x_ring.slots[0][0][:, :, :].rearrange("p c n -> p (c n)"), 4096, True)
        AX1 = Arena(x_ring.slots[1][0][:, :, :].rearrange("p c n -> p (c n)"), 4096, True)

        pti_t = A1.take(128, 4096, I32); pti_b = Buf("pti")
        ptf_t = A1.take(128, 4096, F32); ptf_b = Buf("ptf")
        own4_t = A1.take(128, 640, BF16); own4_b = Buf("own4")
        qm_t = A1.take(128, 2048, BF16).rearrange("p (t n) -> p t n", t=4); qm_b = Buf("qm")
        lfk_t = AQ.take(128, 5120, F32).rearrange("p (g h) -> p g h", h=8); lfk_b = Buf("lfk")
        ob_t = AQ.take(128, 2080, F32); ob_b = Buf("ob")
        tb_t = AQ.take(128, 256, F32); tb_b = Buf("tb")
        wsl_t = AQ.take(128, 256, F32); wsl_b = Buf("wsl")
        wpg_t = AQ.take(128, 40, F32).rearrange("p (c h) -> p c h", h=8); wpg_b = Buf("wpg")
        tloc_t = AQ.take(128, 40, F32).rearrange("p (c h) -> p c h", h=8); tloc_b = Buf("tloc")
        ownT_t = AQ.take(128, 160, F32).rearrange("p (c b) -> p c b", b=32); ownT_b = Buf("ownT")
        ptTf_t = AQ.take(128, 32, F32); ptTf_b = Buf("ptTf")
        ptTi_t = AQ.take(128, 32, I32); ptTi_b = Buf("ptTi")
        pgid_t = AQ.take(128, 5, F32); pgid_b = Buf("pgid")
        tmask_t = AQ.take(128, 4, F32); tmask_b = Buf("tmask")
        hmask_t = AQ.take(128, 2, F32); hmask_b = Buf("hmask")
        dmask_t = AQ.take(128, 8, F32); dmask_b = Buf("dmask")
        AQ.take(128, 1, F32)
        lrow_t = AQ.take(1, 128, F32); lrow_b = Buf("lrow")
        pgrow_t = AM.take(128, 640, F32); pgrow_b = Buf("pgrow")
        util_t = AM.take(128, 128, F32); util_b = Buf("utri")
        tb8_t = AM.take(128, 2048, F32).rearrange("p (r n) -> p r n", r=8); tb8_b = Buf("tb8")
        lfg_t = AM.take(128, 1024, F32); lfg_b = Buf("lfg")
        cmp_ring = Ring([(AM.take(128, 128, F32), Buf("cmp%d" % i)) for i in range(2)])
        qblk_t = A2.take(128, 10240, BF16).rearrange("p (q g e) -> p q g e", q=4, e=8); qblk_b = Buf("qblk")
        orep_t = A2.take(128, 128, BF16).rearrange("p (t b) -> p t b", t=4); orep_b = Buf("orep")
        pt4_ring = Ring([(A2.take(128, 128, BF16), Buf("pt4_%d" % i)) for i in range(2)])
        stmp_ring = Ring([(A2.take(128, 128, F32), Buf("stmp%d" % i)) for i in range(2)])
        od_ring = Ring([(A2.take(128, 64, F32), Buf("od%d" % i)) for i in range(2)])
        kt_slots = Ring([(AX0.take(128, 2048, BF16).rearrange("p (g q k) -> p g q k", g=4, q=4), Buf("kt%d" % i)) for i in range(2)])
        vt_slots = Ring([(AX0.take(128, 2048, BF16).rearrange("p (g n) -> p g n", g=4), Buf("vt%d" % i)) for i in range(2)])
        pv_t = AX1.take(128, 512, F32); pv_b = Buf("pvtmp")
        qn_t = AX1.take(32, 2048, F32).rearrange("p (t n) -> p t n", t=4); qn_b = Buf("qn")

        cg = []
        for (t_, b_, d_) in ((pgid_t, pgid_b, pgid_d), (tmask_t, tmask_b, tmask_d), (hmask_t, hmask_b, hmask_d),
                             (dmask_t, dmask_b, dmask_d), (pgrow_t, pgrow_b, pgrow_d), (util_t, util_b, utri_d),
                             (ptTi_t, ptTi_b, ptT_d)):
            cg.append(("sync", (lambda e, t_=t_, d_=d_: e.dma_start(out=t_, in_=d_.ap())), b_))
        cg.append(("sync", (lambda e: e.dma_start(out=pti_t, in_=ptflat_d.ap().partition_broadcast(128))), pti_b))
        P.const_group(cg)
        P.op("vector", lambda e: e.tensor_copy(out=ptf_t, in_=pti_t), reads=[pti_b], writes=[ptf_b])
        P.op("vector", lambda e: e.tensor_copy(out=ptTf_t, in_=ptTi_t), reads=[ptTi_b], writes=[ptTf_b])
        for c in range(NCH):
            P.dma("sync", lambda e, c=c: e.dma_start(out=lfg_t, in_=lfg_d[:, c, :]), lfg_b)
            P.op("vector", lambda e, c=c: e.tensor_reduce(out=tloc_t[:, c, :], in_=lfg_t.rearrange("p (k h) -> p h k", h=8),
                                                          axis=mybir.AxisListType.X, op=ALU.add),
                 reads=[lfg_b], writes=[tloc_b])
        for b in range(32):
            for c in range(NCH):
                (cm, cmb) = cmp_ring.next()
                P.op("vector", lambda e, b=b, c=c, cm=cm: e.tensor_scalar(
                    out=cm, in0=ptf_t[:, b * 128:(b + 1) * 128], scalar1=pgid_t[:, c:c + 1], scalar2=0.0,
                    op0=ALU.is_equal, op1=ALU.add, accum_out=ownT_t[:, c, b:b + 1]),
                     reads=[ptf_b, pgid_b], writes=[cmb, ownT_b])
                P.op("tensor", lambda e, b=b, c=c, cm=cm: e.matmul(
                    psS_t[:, b * 8:(b + 1) * 8], lhsT=cm, rhs=tloc_t[:, c, :], start=(c == 0), stop=(c == NCH - 1)),
                     reads=[cmb, tloc_b], writes=[psS_b])
        (pyf, pyfb) = psY.next()
        tokf = P.op("tensor", lambda e: e.matmul(pyf[0:32, 0:8], lhsT=ident_t[:, 0:32], rhs=ident_t[:, 0:8], start=True, stop=True),
                    reads=[ident_b], writes=[pyfb])
        psS_b.w = tokf
        P.op("vector", lambda e: e.tensor_copy(out=tb_t, in_=psS_t[:, 0:256]), reads=[psS_b], writes=[tb_b])
        P.dma("sync", lambda e: e.dma_start(out=tbp_in.ap(), in_=tb_t), tbp_inb, reads=[tb_b])
        RG8 = [list(range(8))]
        P.cc(lambda e: e.collective_compute("AllGather", ALU.bypass, replica_groups=RG8,
                                            ins=[tbp_in.ap().opt()], outs=[tbp_out.ap().opt()]), tbp_outb, reads=[tbp_inb])
        P.dma("sync", lambda e: e.dma_start(out=tb8_t, in_=tbp_out.ap().rearrange("(r p) n -> p r n", r=8)), tb8_b, reads=[tbp_outb])
        P.op("vector", lambda e: e.tensor_reduce(out=tb_t, in_=tb8_t.rearrange("p r n -> p n r"),
                                                 axis=mybir.AxisListType.X, op=ALU.add), reads=[tb8_b], writes=[tb_b])
        P.op("tensor", lambda e: e.matmul(psM_t[:, 0:256], lhsT=util_t, rhs=tb_t, start=True, stop=True),
             reads=[util_b, tb_b], writes=[psM_b])
        P.op("scalar", lambda e: e.activation(out=wsl_t, in_=psM_t[:, 0:256], func=AF.Exp), reads=[psM_b], writes=[wsl_b])
        for c in range(NCH):
            (py, pyb) = psY.next()
            for b in range(32):
                (cm, cmb) = cmp_ring.next()
                P.op("vector", lambda e, b=b, c=c, cm=cm: e.tensor_scalar(
                    out=cm, in0=pgrow_t[:, c * 128:(c + 1) * 128], scalar1=ptTf_t[:, b:b + 1], scalar2=None, op0=ALU.is_equal),
                     reads=[pgrow_b, ptTf_b], writes=[cmb])
                P.op("tensor", lambda e, b=b, cm=cm, py=py: e.matmul(
                    py[:, 0:8], lhsT=cm, rhs=wsl_t[:, b * 8:(b + 1) * 8], start=(b == 0), stop=(b == 31)),
                     reads=[cmb, wsl_b], writes=[pyb])
            pe_fence([pyb])
            P.op("vector", lambda e, c=c, py=py: e.tensor_copy(out=wpg_t[:, c, :], in_=py[:, 0:8]), reads=[pyb], writes=[wpg_b])
        for c in range(NCH):
            for t in range(4):
                P.op("vector", lambda e, c=c, t=t: e.tensor_copy(out=orep_t[:, t, :], in_=ownT_t[:, c, :]),
                     reads=[ownT_b], writes=[orep_b])
            (py, pyb) = psY.next()
            P.op("tensor", lambda e, py=py: e.matmul(py[:, 0:128], lhsT=orep_t[:, :, :].rearrange("p t b -> p (t b)"),
                                                     rhs=ident_t[:, :], start=True, stop=True),
                 reads=[orep_b, ident_b], writes=[pyb])
            P.op("vector", lambda e, c=c, py=py: e.tensor_copy(out=own4_t[:, c * 128:(c + 1) * 128], in_=py[:, 0:128]),
                 reads=[pyb], writes=[own4_b])
        for t in range(4):
            P.op("vector", lambda e, t=t: e.tensor_scalar(out=qm_t[:, t, :], in0=qtok_t[:, :], scalar1=tmask_t[:, t:t + 1],
                                                          scalar2=None, op0=ALU.mult),
                 reads=[qtok_b, tmask_b], writes=[qm_b])
        P.dma("sync", lambda e: e.dma_start(out=lfk_t, in_=lfk_d.ap()), lfk_b)
        lfk_flat = lfk_t.rearrange("p g h -> p (g h)")
        for cb in range(10):
            (py, pyb) = psY.next()
            P.op("tensor", lambda e, cb=cb, py=py: e.matmul(py[:, :], lhsT=util_t, rhs=lfk_flat[:, cb * 512:(cb + 1) * 512],
                                                            start=True, stop=True),
                 reads=[util_b, lfk_b], writes=[pyb])
            P.op("vector", lambda e, cb=cb, py=py: e.tensor_copy(out=lfk_flat[:, cb * 512:(cb + 1) * 512], in_=py[:, :]),
                 reads=[pyb], writes=[lfk_b])
        od_flat = od_d.ap().rearrange("g r d -> (g r) d")
        for half in range(2):
            g0h = half * 320
            for pair in range(4):
                for t in range(4):
                    (pq, pqb) = psQK.next()
                    P.op("tensor", lambda e, pair=pair, t=t, pq=pq, g0h=g0h: e.matmul(
                        pq[:, 0:320], lhsT=qm_t[:, t, pair * 128:(pair + 1) * 128], rhs=own4_t[:, g0h:g0h + 320],
                        start=True, stop=True), reads=[qm_b, own4_b], writes=[pqb])
                    pe_fence([pqb])
                    for h2 in range(2):
                        P.op("vector", lambda e, pair=pair, t=t, h2=h2, pq=pq: e.tensor_scalar(
                            out=qblk_t[:, pair, :, h2 * 4 + t], in0=pq[:, 0:320], scalar1=hmask_t[:, h2:h2 + 1],
                            scalar2=None, op0=ALU.mult), reads=[pqb, hmask_b], writes=[qblk_b])
            for gl in range(80):
                grp = half * 80 + gl
                g0 = grp * 4
                (kt, ktb) = kt_slots.next()
                (vt, vtb) = vt_slots.next()
                P.dma("gpsimd", lambda e, grp=grp, kt=kt: e.dma_start(out=kt, in_=kpT_d[grp], max_dma_last_dim=4096), ktb)
                P.dma("gpsimd", lambda e, grp=grp, vt=vt: e.dma_start(out=vt, in_=vp_d[grp], max_dma_last_dim=4096), vtb)
                (pq, pqb) = psQK.next()
                for pg in range(4):
                    for pair in range(4):
                        P.op("tensor", lambda e, pg=pg, pair=pair, pq=pq, kt=kt, gl=gl: e.matmul(
                            pq[:, pg * 32 + pair * 8:pg * 32 + pair * 8 + 8], lhsT=kt[:, pg, pair, :],
                            rhs=qblk_t[:, pair, gl * 4 + pg, :], start=True, stop=True),
                             reads=[ktb, qblk_b], writes=[pqb])
                pe_fence([pqb])
                (st, stb) = stmp_ring.next()
                for t in range(4):
                    P.op("vector", lambda e, t=t, pq=pq, st=st, g0=g0: e.tensor_tensor(
                        out=st.rearrange("p (q t) -> p q t", t=4)[:, :, t], in0=pq[:, 0:128].rearrange("p (q t) -> p q t", t=4)[:, :, t],
                        in1=lfk_t[:, g0:g0 + 4, :].rearrange("p g h -> p (g h)"), op=ALU.add),
                         reads=[pqb, lfk_b], writes=[stb])
                (p4, p4b) = pt4_ring.next()
                P.op("scalar", lambda e, st=st, p4=p4: e.activation(out=p4, in_=st, func=AF.Exp), reads=[stb], writes=[p4b])
                P.op("tensor", lambda e, p4=p4: e.matmul(psM_t[0:1, 0:128], lhsT=ones_t[:, 0:1], rhs=p4, start=True, stop=True),
                     reads=[ones_b, p4b], writes=[psM_b])
                pe_fence([psM_b])
                P.op("vector", lambda e: e.tensor_copy(out=lrow_t, in_=psM_t[0:1, 0:128]), reads=[psM_b], writes=[lrow_b])
                P.dma("sync", lambda e, g0=g0: e.dma_start(out=ld_d[g0:g0 + 4, :].rearrange("g r -> (g r)").rearrange("(o n) -> o n", o=1),
                                                           in_=lrow_t), ld_db, reads=[lrow_b], src=lrow_b)
                (po, pob) = psY.next()
                for pg in range(4):
                    P.op("tensor", lambda e, pg=pg, po=po, p4=p4, vt=vt: e.matmul(
                        po[pg * 32:(pg + 1) * 32, :], lhsT=p4[:, pg * 32:(pg + 1) * 32], rhs=vt[:, pg, :], start=True, stop=True, tile_position=(0, pg * 32)),
                         reads=[p4b, vtb], writes=[pob])
                pe_fence([pob])
                P.op("vector", lambda e, po=po: e.tensor_tensor(
                    out=pv_t.rearrange("p (h d) -> p h d", h=8), in0=po[:, :].rearrange("p (h d) -> p h d", h=8),
                    in1=dmask_t.unsqueeze(2).to_broadcast([128, 8, 64]), op=ALU.mult),
                     reads=[pob, dmask_b], writes=[pv_b])
                (od, odb) = od_ring.next()
                P.op("vector", lambda e, od=od: e.tensor_reduce(out=od, in_=pv_t.rearrange("p (h d) -> p d h", h=8),
                                                                axis=mybir.AxisListType.X, op=ALU.add),
                     reads=[pv_b], writes=[odb])
                P.dma("sync", lambda e, g0=g0, od=od: e.dma_start(out=od_flat[g0 * 32:(g0 + 4) * 32, :], in_=od), od_db, reads=[odb], src=odb)
        part_ps = [psA.slots[0], psA.slots[1], psB.slots[0], psB.slots[1], psY.slots[0]]
        colblk = [(0, 512), (512, 512), (1024, 512), (1536, 512), (2048, 32)]
        for c in range(NCH):
            P.dma("sync", lambda e, c=c: e.dma_start(out=ob_t[:, 0:2048], in_=od_d[c * 128:(c + 1) * 128].rearrange("g r d -> g (r d)")),
                  ob_b, reads=[od_db])
            P.dma("sync", lambda e, c=c: e.dma_start(out=ob_t[:, 2048:2080], in_=ld_d[c * 128:(c + 1) * 128, :]), ob_b, reads=[ld_db])
            for h in range(8):
                P.op("vector", lambda e, c=c, h=h: e.tensor_scalar(
                    out=ob_t[:, h * 256:(h + 1) * 256], in0=ob_t[:, h * 256:(h + 1) * 256], scalar1=wpg_t[:, c, h:h + 1],
                    scalar2=None, op0=ALU.mult), reads=[ob_b, wpg_b], writes=[ob_b])
                P.op("vector", lambda e, c=c, h=h: e.tensor_scalar(
                    out=ob_t[:, 2048 + h * 4:2048 + (h + 1) * 4], in0=ob_t[:, 2048 + h * 4:2048 + (h + 1) * 4],
                    scalar1=wpg_t[:, c, h:h + 1], scalar2=None, op0=ALU.mult), reads=[ob_b, wpg_b], writes=[ob_b])
            for k, (c0, w) in enumerate(colblk):
                (pp, ppb) = part_ps[k]
                P.op("tensor", lambda e, c=c, pp=pp, c0=c0, w=w: e.matmul(
                    pp[0:32, 0:w], lhsT=ownT_t[:, c, :], rhs=ob_t[:, c0:c0 + w], start=(c == 0), stop=(c == NCH - 1)),
                     reads=[ownT_b, ob_b], writes=[ppb])
        P.barrier_all()
        A1.off = 0
        kn_t = A1.take(32, 2048, F32).rearrange("p (t n) -> p t n", t=4); kn_b = Buf("kn")
        vn_t = A1.take(32, 2048, F32).rearrange("p (t n) -> p t n", t=4); vn_b = Buf("vn")
        acc_t = A1.take(32, 2080, F32); acc_b = Buf("acc")
        rk_ring = Ring([(A1.take(32, 2080, F32), Buf("rk%d" % i)) for i in range(1)])
        id32_t = A1.take(32, 32, F32); id32_b = Buf("id32")
        P.dma("sync", lambda e: e.dma_start(out=id32_t, in_=id32_d.ap()), id32_b)
        for k, (c0, w) in enumerate(colblk):
            (pp, ppb) = part_ps[k]
            P.op("vector", lambda e, pp=pp, c0=c0, w=w: e.tensor_copy(out=acc_t[:, c0:c0 + w], in_=pp[0:32, 0:w]),
                 reads=[ppb], writes=[acc_b])
        P.dma("sync", lambda e: e.dma_start(out=part_in.ap(), in_=acc_t), part_inb, reads=[acc_b])
        P.cc(lambda e: e.collective_compute("AllGather", ALU.bypass, replica_groups=RG8,
                                            ins=[part_in.ap().opt()], outs=[part_out.ap().opt()]), part_outb, reads=[part_inb])
        for r in range(8):
            (rk, rkb) = rk_ring.next()
            P.dma("sync", lambda e, r=r, rk=rk: e.dma_start(out=rk, in_=part_out[r * 32:(r + 1) * 32, :]), rkb, reads=[part_outb])
            if r == 0:
                P.op("vector", lambda e, rk=rk: e.tensor_copy(out=acc_t, in_=rk), reads=[rkb], writes=[acc_b])
            else:
                P.op("vector", lambda e, rk=rk: e.tensor_tensor(out=acc_t, in0=acc_t, in1=rk, op=ALU.add),
                     reads=[rkb, acc_b], writes=[acc_b])
        if DBG:
            P.dma("sync", lambda e: e.dma_start(out=dbg_ld.ap(), in_=ld_d.ap()), dbg_ldb, reads=[ld_db])
            P.dma("sync", lambda e: e.dma_start(out=dbg_od.ap(), in_=od_d.ap().rearrange("g r d -> g (r d)")), dbg_odb, reads=[od_db])
            P.dma("sync", lambda e: e.dma_start(out=dbg_wpg.ap(), in_=wpg_t.rearrange("p c h -> p (c h)")), dbg_wpgb, reads=[wpg_b])
            P.dma("sync", lambda e: e.dma_start(out=dbg_own.ap(), in_=ownT_t.rearrange("p c b -> p (c b)")), dbg_ownb, reads=[ownT_b])
            P.dma("sync", lambda e: e.dma_start(out=dbg_acc.ap(), in_=acc_t), dbg_accb, reads=[acc_b])
            P.dma("sync", lambda e: e.dma_start(out=dbg_w.ap(), in_=wsl_t), dbg_wb, reads=[wsl_b])
        P.dma("sync", lambda e: e.dma_start(out=qn_t, in_=qs_d.ap().rearrange("(t b) n -> b t n", t=4)), qn_b, reads=[qs_db])
        P.dma("sync", lambda e: e.dma_start(out=kn_t, in_=ks_d.ap().rearrange("(t b) n -> b t n", t=4)), kn_b, reads=[ks_db])
        P.dma("sync", lambda e: e.dma_start(out=vn_t, in_=v_o[NPT:NT, :].rearrange("(t b) n -> b t n", t=4)), vn_b, reads=[v_ob])
        lfn_t = A1.take(32, 32, F32).rearrange("p (t h) -> p t h", t=4); lfn_b = Buf("lfn")
        sn_t = A1.take(32, 80, F32).rearrange("p (q h) -> p q h", h=8); sn_b = Buf("sn")
        for t in range(4):
            P.dma("sync", lambda e, t=t: e.dma_start(out=lfn_t[:, t, :], in_=lfT_o[:, NPT + t * 32:NPT + (t + 1) * 32].rearrange("h b -> b h"),
                                                     allow_slow_non_contiguous=True), lfn_b, reads=[lfT_ob])
        for t in range(1, 4):
            P.op("vector", lambda e, t=t: e.tensor_tensor(out=lfn_t[:, t, :], in0=lfn_t[:, t, :], in1=lfn_t[:, t - 1, :], op=ALU.add),
                 reads=[lfn_b], writes=[lfn_b])
        pairs = [(t, tp) for t in range(4) for tp in range(t + 1)]
        prod_t = pv_t[0:32, :].rearrange("p (h d) -> p h d", h=8)
        for qi, (t, tp) in enumerate(pairs):
            P.op("vector", lambda e, t=t, tp=tp: e.tensor_tensor(
                out=prod_t, in0=qn_t[:, t, :].rearrange("p (h d) -> p h d", h=8), in1=kn_t[:, tp, :].rearrange("p (h d) -> p h d", h=8),
                op=ALU.mult), reads=[qn_b, kn_b], writes=[pv_b])
            P.op("vector", lambda e, qi=qi: e.tensor_reduce(out=sn_t[:, qi, :], in_=prod_t, axis=mybir.AxisListType.X, op=ALU.add),
                 reads=[pv_b], writes=[sn_b])
            P.op("vector", lambda e, qi=qi, tp=tp: e.tensor_tensor(out=sn_t[:, qi, :], in0=sn_t[:, qi, :], in1=lfn_t[:, tp, :], op=ALU.subtract),
                 reads=[sn_b, lfn_b], writes=[sn_b])
        P.op("scalar", lambda e: e.activation(out=sn_t, in_=sn_t, func=AF.Exp), reads=[sn_b], writes=[sn_b])
        acc_o = acc_t[:, 0:2048].rearrange("p (h t d) -> p h t d", h=8, t=4)
        acc_l = acc_t[:, 2048:2080].rearrange("p (h t) -> p h t", t=4)
        for qi, (t, tp) in enumerate(pairs):
            P.op("vector", lambda e, qi=qi, tp=tp: e.tensor_tensor(
                out=prod_t, in0=vn_t[:, tp, :].rearrange("p (h d) -> p h d", h=8),
                in1=sn_t[:, qi, :].unsqueeze(2).to_broadcast([32, 8, 64]), op=ALU.mult),
                 reads=[vn_b, sn_b], writes=[pv_b])
            P.op("vector", lambda e, t=t: e.tensor_tensor(out=acc_o[:, :, t, :], in0=acc_o[:, :, t, :], in1=prod_t, op=ALU.add),
                 reads=[pv_b, acc_b], writes=[acc_b])
            P.op("vector", lambda e, qi=qi, t=t: e.tensor_tensor(out=acc_l[:, :, t], in0=acc_l[:, :, t], in1=sn_t[:, qi, :], op=ALU.add),
                 reads=[sn_b, acc_b], writes=[acc_b])
        P.op("vector", lambda e: e.reciprocal(out=acc_t[:, 2048:2080], in_=acc_t[:, 2048:2080]), reads=[acc_b], writes=[acc_b])
        for h in range(8):
            P.op("vector", lambda e, h=h: e.tensor_tensor(
                out=acc_o[:, h, :, :], in0=acc_o[:, h, :, :], in1=acc_l[:, h, :].unsqueeze(2).to_broadcast([32, 4, 64]), op=ALU.mult),
                 reads=[acc_b], writes=[acc_b])
        if DBG:
            P.dma("sync", lambda e: e.dma_start(out=dbg_o.ap(), in_=acc_t), dbg_ob, reads=[acc_b])
        for h in range(8):
            for t in range(4):
                (py, pyb) = psY.next()
                P.op("tensor", lambda e, h=h, t=t, py=py: e.matmul(
                    py[0:64, 0:32], lhsT=acc_o[:, h, t, :], rhs=id32_t, start=True, stop=True),
                     reads=[acc_b, id32_b], writes=[pyb])
                (od, odb) = od_ring.next()
                P.op("vector", lambda e, py=py, od=od: e.tensor_copy(out=od[0:64, 0:32], in_=py[0:64, 0:32]), reads=[pyb], writes=[odb])
                P.dma("sync", lambda e, h=h, t=t, od=od: e.dma_start(out=oat_d[h, :, NPT + t * 32:NPT + (t + 1) * 32], in_=od[0:64, 0:32]),
                      oat_b, reads=[odb], src=odb)
        P.barrier_all()
        phase_c(NPT, NS, None)

        P.finish(out_bufs)
        P.build()
    return nc


SAMPLE_PHASE_C = False
STAGE = 9


def _chunk_rows(w, kc):
    F = w.shape[1]
    return np.ascontiguousarray(w.reshape(kc, 128, F).transpose(1, 0, 2))


def _prep_shared(inp):
    sh = {}
    for l, sfx in enumerate(("ffn1", "ffn2")):
        w1 = inp["w1_" + sfx][0]
        w3 = inp["w3_" + sfx][0]
        w2 = inp["w2_" + sfx][0]
        sh["w1_%d" % l] = np.ascontiguousarray(w1.reshape(KC, 128, FC, 128).transpose(2, 1, 0, 3))
        sh["w3_%d" % l] = np.ascontiguousarray(w3.reshape(KC, 128, FC, 128).transpose(2, 1, 0, 3))
        sh["w2_%d" % l] = np.ascontiguousarray(w2.reshape(FC, 128, KC, 128).transpose(2, 1, 0, 3))
    w_in = inp["w_in"][0]
    sh["win_qk"] = np.ascontiguousarray(w_in[:, 0:1024].reshape(KC, 128, 16, 64).transpose(2, 1, 0, 3))
    sh["win_v"] = _chunk_rows(w_in[:, 1024:1536], KC)
    sh["win_f"] = _chunk_rows(w_in[:, 1536:1544], KC)
    sh["win_c"] = np.ascontiguousarray(w_in[:, 1544:3080].reshape(KC, 128, 12, 128).transpose(2, 1, 0, 3))
    w_out = inp["w_out"][0]
    sh["wo_a"] = np.ascontiguousarray(w_out[0:512].reshape(8, 64, KC, 128).transpose(2, 1, 0, 3))
    sh["wo_c"] = np.ascontiguousarray(w_out[512:1024].reshape(4, 128, KC, 128).transpose(2, 1, 0, 3))
    g = np.stack([inp["g_ffn1"][0], inp["g_mix"][0], inp["g_ffn2"][0], inp["g_final"]], 0)
    sh["gains"] = np.ascontiguousarray(g.reshape(4, KC, 128).transpose(2, 0, 1))
    sh["bfn"] = np.ascontiguousarray(inp["b_f"][0].reshape(8, 1))
    sh["gao"] = np.ascontiguousarray(inp["g_attn_out"][0].reshape(8, 64).T)
    sh["gco"] = np.ascontiguousarray(inp["g_conv_out"][0].reshape(4, 128).T)
    sh["convw"] = np.ascontiguousarray(inp["conv_w"][0].reshape(3, 4, 128).transpose(2, 1, 0))
    sh["ident"] = np.eye(128, dtype=np.float32)
    s65 = np.zeros((65, 64), np.float32)
    s65[64, :] = 1.0
    sh["sel65"] = s65
    sc = inp["state_conv"][0]
    sh["sconvT"] = np.ascontiguousarray(sc.reshape(32, 2, 4, 128).transpose(3, 2, 1, 0).reshape(128, 4, 64))
    xs = inp["x_sample"].transpose(1, 0, 2).reshape(NS, D)
    sh["_xsT"] = xs.T
    wtm = np.stack([w_in[:, 0:256], w_in[:, 256:512], w_in[:, 512:768], w_in[:, 768:1024]], 0)
    sh["win_tm"] = np.ascontiguousarray(wtm.reshape(4, KC, 128, 256).transpose(0, 2, 1, 3))
    p = np.arange(128)
    sh["tmask"] = (p[:, None] // 32 == np.arange(4)[None, :]).astype(np.float32)
    sh["hmask"] = (p[:, None] // 64 == np.arange(2)[None, :]).astype(np.float32)
    sh["dmask"] = ((p[:, None] % 32) // 4 == np.arange(8)[None, :]).astype(np.float32)
    sh["utri"] = (p[:, None] > p[None, :]).astype(np.float32)
    sh["id32"] = np.eye(32, dtype=np.float32)
    pt = inp["page_table"].astype(np.int32)
    sh["ptT"] = np.ascontiguousarray(pt.T)
    sh["ptflat"] = np.ascontiguousarray(pt.reshape(1, -1))
    return sh


def _pool_shard(inp, i):
    sl = slice(640 * i, 640 * (i + 1))
    ck = inp["cache_k"][0, sl]
    cv = inp["cache_v"][0, sl]
    cl = inp["cache_logf"][0, sl]
    m = {}
    m["kpT"] = np.ascontiguousarray(ck.reshape(160, 4, 128, 4, 2, 64).transpose(0, 4, 5, 1, 3, 2)).reshape(160, 128, 4, 4, 128)
    m["vp"] = np.ascontiguousarray(cv.reshape(160, 4, 128, 512).transpose(0, 2, 1, 3))
    m["lfk"] = np.ascontiguousarray(cl.transpose(1, 0, 2))
    m["lfg"] = np.ascontiguousarray(cl.reshape(5, 128, 1024).transpose(1, 0, 2))
    pid = (640 * i + np.arange(640)).astype(np.float32)
    m["pgid"] = np.ascontiguousarray(pid.reshape(5, 128).T)
    m["pgrow"] = np.ascontiguousarray(np.broadcast_to(pid[None, :], (128, 640)))
    return m


def _core_consts(j):
    selj = np.zeros((128, 4), np.float32)
    selj[:, j] = 1.0
    hsel = np.zeros((128, NSL, 16), np.float32)
    for s in range(NSL):
        if j > 0:
            hsel[:, s, (j - 1) * 4 + s] = 1.0
        elif s > 0:
            hsel[:, s, 3 * 4 + (s - 1)] = 1.0
    masks = np.zeros((128, 16, 512), np.float32)
    kk = np.arange(128)[:, None]
    qq = np.arange(512)[None, :]
    for kr in range(4):
        for kb in range(4):
            if kr > j:
                masks[:, kr * 4 + kb, :] = NEG
            elif kr == j:
                masks[:, kr * 4 + kb, :] = np.where(kb * 128 + kk <= qq, 0.0, NEG)
    return selj, hsel, masks


def _core_inputs(inp, c, sh):
    b, j = c // 4, c % 4
    xp = inp["x_prompt"][b].reshape(4, 4, 512, D)[:, j].reshape(NPT, D)
    xTfull = np.concatenate([xp.T, sh["_xsT"]], 1)
    m = {k: v for k, v in sh.items() if not k.startswith("_")}
    m["xT"] = np.ascontiguousarray(xTfull.reshape(KC, 128, NT).transpose(1, 0, 2))
    m["selj"], m["hsel"], m["masks"] = _core_consts(j)
    m.update(_pool_shard(inp, c))
    return m


_NC_CACHE = {}


def _assemble(res):
    B, S = 2, 8192
    k_p = np.zeros((1, B, S, 8, 64), np.float32)
    v_p = np.zeros((1, B, S, 8, 64), np.float32)
    f_p = np.zeros((1, B, S, 8), np.float32)
    c_p = np.zeros((1, B, 2, 512), np.float32)
    y_p = np.zeros((B, S, D), np.float32)
    for c in range(NCORES):
        b, j = c // 4, c % 4
        r = res[c]
        kT = r["kT_o"].reshape(512, NT)
        ucT = r["ucT_o"].transpose(1, 0, 2).reshape(512, NT)
        yT = r["yT_o"].transpose(1, 0, 2).reshape(D, NT)
        for s in range(NSL):
            g0 = (4 * s + j) * 512
            sl = slice(s * 512, (s + 1) * 512)
            k_p[0, b, g0:g0 + 512] = kT[:, sl].T.reshape(512, 8, 64)
            v_p[0, b, g0:g0 + 512] = r["v_o"][sl].reshape(512, 8, 64)
            f_p[0, b, g0:g0 + 512] = r["lfT_o"][:, sl].T
            y_p[b, g0:g0 + 512] = yT[:, sl].T
        if j == 3:
            c_p[0, b] = ucT[:, NPT - 2:NPT].T
    r = res[0]
    kT = r["kT_o"].reshape(512, NT)[:, NPT:]
    ucT = r["ucT_o"].transpose(1, 0, 2).reshape(512, NT)[:, NPT:]
    yT = r["yT_o"].transpose(1, 0, 2).reshape(D, NT)[:, NPT:]
    k_s = np.ascontiguousarray(kT.T.reshape(4, 32, 8, 64).transpose(1, 0, 2, 3))[None]
    v_s = np.ascontiguousarray(r["v_o"][NPT:].reshape(4, 32, 8, 64).transpose(1, 0, 2, 3))[None]
    f_s = np.ascontiguousarray(r["lfT_o"][:, NPT:].T.reshape(4, 32, 8).transpose(1, 0, 2))[None]
    c_s = np.ascontiguousarray(ucT.T.reshape(4, 32, 512).transpose(1, 0, 2)[:, 2:4])[None]
    y_s = np.ascontiguousarray(yT.T.reshape(4, 32, D).transpose(1, 0, 2))
    return (y_p, y_s, k_p, v_p, f_p, c_p, k_s, v_s, f_s, c_s)


def kernel(**inp):
    inp = {k: np.asarray(v) for k, v in inp.items()}
    if "nc" not in _NC_CACHE:
        _NC_CACHE["nc"] = build_program()
    nc = _NC_CACHE["nc"]
    sh = _prep_shared(inp)
    in_maps = [_core_inputs(inp, c, sh) for c in range(NCORES)]
    res = run_bass_kernel_spmd(nc, in_maps, core_ids=list(range(NCORES))).results
    return _assemble(res)
```
